# Optimizing a Trainium2 kernel written in Bass

```python
import jax, jax.numpy as jnp
from jax import lax
import numpy as np

D_MODEL = 4096
BATCH = 2
SEQ = 8192
DEPTH = 2

CHUNK = 64
Q_BLOCK = 128

MIX_WIDTH = D_MODEL
HEAD_DIM = 128
SB_WIDTH = 3 * MIX_WIDTH // 8
CONV_WIDTH = MIX_WIDTH // 4
GLA_WIDTH = MIX_WIDTH - SB_WIDTH - CONV_WIDTH
SB_HEADS = SB_WIDTH // HEAD_DIM
SB_SCALE = HEAD_DIM ** -0.5
GLA_HEADS = 6
GLA_DV = GLA_WIDTH // GLA_HEADS
GLA_DK = GLA_DV // 2
GLA_KEY_WIDTH = GLA_HEADS * GLA_DK
GLA_RANK = 16
GLA_TAU = 16.0
GLA_SCALE = GLA_DK ** -0.5
CONV_K = 31
D_FF = ((8 * D_MODEL // 3 + 255) // 256) * 256
EPS = 1e-6

SPLIT_SIZES = (SB_WIDTH, SB_WIDTH, SB_WIDTH,
               CONV_WIDTH, CONV_WIDTH,
               GLA_KEY_WIDTH, GLA_KEY_WIDTH,
               GLA_WIDTH, GLA_WIDTH,
               GLA_RANK)
IN_WIDTH = sum(SPLIT_SIZES)

kernel_name = "hybrid_sb_conformer_gla_macaron"


def rmsnorm(x, g):
    xf = x.astype(jnp.float32)
    y = xf * lax.rsqrt(jnp.mean(xf * xf, axis=-1, keepdims=True) + EPS)
    return (y * g.astype(jnp.float32)).astype(x.dtype)


def layernorm(x, g, b):
    xf = x.astype(jnp.float32)
    mu = jnp.mean(xf, axis=-1, keepdims=True)
    var = jnp.mean(jnp.square(xf - mu), axis=-1, keepdims=True)
    y = (xf - mu) * lax.rsqrt(var + EPS)
    return (y * g.astype(jnp.float32) + b.astype(jnp.float32)).astype(x.dtype)


def swiglu_ffn(h, w_in, w_out):
    gate, up = jnp.split(h @ w_in, 2, axis=-1)
    return (jax.nn.silu(gate) * up) @ w_out


def stick_breaking_attention(q, k, v):
    B, H, S, Dh = q.shape
    nb = S // Q_BLOCK
    qb = q.reshape(B, H, nb, Q_BLOCK, Dh).transpose(2, 0, 1, 3, 4)
    key_pos = jnp.arange(S)

    def block(args):
        qi, bi = args
        z = jnp.einsum('bhqd,bhkd->bhqk', qi, k,
                       preferred_element_type=jnp.float32) * SB_SCALE
        q_pos = bi * Q_BLOCK + jnp.arange(Q_BLOCK)
        mask = key_pos[None, :] < q_pos[:, None]
        log_keep = jnp.where(mask, jax.nn.log_sigmoid(-z), 0.0)
        after = lax.cumsum(log_keep, axis=3, reverse=True) - log_keep
        a = jnp.where(mask, jnp.exp(jax.nn.log_sigmoid(z) + after), 0.0)
        return jnp.einsum('bhqk,bhkd->bhqd', a.astype(v.dtype), v)

    out = lax.map(block, (qb, jnp.arange(nb)))
    return out.transpose(1, 2, 0, 3, 4).reshape(B, H, S, Dh)


def gated_linear_attention(q, k, v, log_a):
    B, H, S, dk = q.shape
    dv = v.shape[-1]
    n = S // CHUNK

    def to_chunks(t):
        return t.astype(jnp.float32).reshape(B, H, n, CHUNK, t.shape[-1]).transpose(2, 0, 1, 3, 4)

    qc, kc, vc, gc = to_chunks(q * GLA_SCALE), to_chunks(k), to_chunks(v), to_chunks(log_a)
    causal = jnp.tril(jnp.ones((CHUNK, CHUNK), dtype=bool))

    def step(state, inp):
        qi, ki, vi, gi = inp
        b = jnp.cumsum(gi, axis=2)
        rel = b[:, :, :, None, :] - b[:, :, None, :, :]
        decay = jnp.exp(jnp.where(causal[None, None, :, :, None], rel, -jnp.inf))
        scores = jnp.einsum('bhtd,bhsd,bhtsd->bhts', qi, ki, decay)
        o = (jnp.einsum('bhts,bhsv->bhtv', scores, vi)
             + jnp.einsum('bhtd,bhdv->bhtv', qi * jnp.exp(b), state))
        b_last = b[:, :, -1:, :]
        state = (jnp.exp(b_last)[:, :, 0, :, None] * state
                 + jnp.einsum('bhsd,bhsv->bhdv', ki * jnp.exp(b_last - b), vi))
        return state, o

    state0 = jnp.zeros((B, H, dk, dv), jnp.float32)
    _, out = lax.scan(step, state0, (qc, kc, vc, gc))
    return out.transpose(1, 2, 0, 3, 4).reshape(B, H, S, dv)


def causal_depthwise_conv(u, w, b):
    C = u.shape[-1]
    y = lax.conv_general_dilated(u, w[:, None, :].astype(u.dtype), window_strides=(1,),
                                 padding=[(CONV_K - 1, 0)],
                                 dimension_numbers=('NWC', 'WIO', 'NWC'),
                                 feature_group_count=C)
    return y + b.astype(u.dtype)


def hybrid_mixer(h, w_in, sb_q_gain, sb_k_gain, conv_w, conv_b, conv_ln_g, conv_ln_b,
                 gla_gate_w, gla_gate_b, gla_out_gain, w_out):
    B, S, _ = h.shape
    offsets = np.cumsum(SPLIT_SIZES)[:-1].tolist()
    (sb_q, sb_k, sb_v, conv_val, conv_gate,
     gla_q, gla_k, gla_v, gla_r, gla_lr) = jnp.split(h @ w_in, offsets, axis=-1)

    def heads(t, nh):
        return t.reshape(B, S, nh, -1).transpose(0, 2, 1, 3)

    def merge(t):
        return t.transpose(0, 2, 1, 3).reshape(B, S, -1)

    qa = rmsnorm(heads(sb_q, SB_HEADS), sb_q_gain)
    ka = rmsnorm(heads(sb_k, SB_HEADS), sb_k_gain)
    ya = merge(stick_breaking_attention(qa, ka, heads(sb_v, SB_HEADS)))

    u = conv_val * jax.nn.sigmoid(conv_gate)
    u = causal_depthwise_conv(u, conv_w, conv_b)
    yb = jax.nn.silu(layernorm(u, conv_ln_g, conv_ln_b))

    gate_logit = (gla_lr @ gla_gate_w + gla_gate_b).astype(jnp.float32)
    log_a = jax.nn.log_sigmoid(gate_logit) / GLA_TAU
    o = gated_linear_attention(heads(gla_q, GLA_HEADS), heads(gla_k, GLA_HEADS),
                               heads(gla_v, GLA_HEADS), heads(log_a, GLA_HEADS))
    o = rmsnorm(o, gla_out_gain)
    yc = (merge(o) * jax.nn.silu(gla_r.astype(jnp.float32))).astype(h.dtype)

    y = jnp.concatenate([ya.astype(h.dtype), yb.astype(h.dtype), yc], axis=-1)
    return y @ w_out


def setup_inputs(seed: int = 0) -> dict:
    key = jax.random.key(seed)
    ks = jax.random.split(key, 24)
    f32 = jnp.float32

    def normal(k, shape, scale):
        return jax.random.normal(k, shape, f32) * scale

    def gain(k, shape):
        return 1.0 + 0.02 * jax.random.normal(k, shape, f32)

    return {
        "x": jax.random.normal(ks[0], (BATCH, SEQ, D_MODEL), f32),
        "ffn1_norm": gain(ks[1], (DEPTH, D_MODEL)),
        "ffn1_w_in": normal(ks[2], (DEPTH, D_MODEL, 2 * D_FF), D_MODEL ** -0.5),
        "ffn1_w_out": normal(ks[3], (DEPTH, D_FF, D_MODEL), D_FF ** -0.5),
        "mix_norm": gain(ks[4], (DEPTH, D_MODEL)),
        "mix_w_in": normal(ks[5], (DEPTH, D_MODEL, IN_WIDTH), D_MODEL ** -0.5),
        "sb_q_gain": gain(ks[6], (DEPTH, HEAD_DIM)),
        "sb_k_gain": gain(ks[7], (DEPTH, HEAD_DIM)),
        "conv_w": normal(ks[8], (DEPTH, CONV_K, CONV_WIDTH), CONV_K ** -0.5),
        "conv_b": normal(ks[9], (DEPTH, CONV_WIDTH), 0.02),
        "conv_ln_g": gain(ks[10], (DEPTH, CONV_WIDTH)),
        "conv_ln_b": normal(ks[11], (DEPTH, CONV_WIDTH), 0.02),
        "gla_gate_w": normal(ks[12], (DEPTH, GLA_RANK, GLA_KEY_WIDTH), GLA_RANK ** -0.5),
        "gla_gate_b": normal(ks[13], (DEPTH, GLA_KEY_WIDTH), 0.1),
        "gla_out_gain": gain(ks[14], (DEPTH, GLA_DV)),
        "mix_w_out": normal(ks[15], (DEPTH, MIX_WIDTH, D_MODEL), MIX_WIDTH ** -0.5),
        "ffn2_norm": gain(ks[16], (DEPTH, D_MODEL)),
        "ffn2_w_in": normal(ks[17], (DEPTH, D_MODEL, 2 * D_FF), D_MODEL ** -0.5),
        "ffn2_w_out": normal(ks[18], (DEPTH, D_FF, D_MODEL), D_FF ** -0.5),
        "final_norm": gain(ks[19], (DEPTH, D_MODEL)),
    }


def reference(x, ffn1_norm, ffn1_w_in, ffn1_w_out, mix_norm, mix_w_in, sb_q_gain, sb_k_gain,
              conv_w, conv_b, conv_ln_g, conv_ln_b, gla_gate_w, gla_gate_b, gla_out_gain,
              mix_w_out, ffn2_norm, ffn2_w_in, ffn2_w_out, final_norm):
    for l in range(DEPTH):
        x = x + 0.5 * swiglu_ffn(rmsnorm(x, ffn1_norm[l]), ffn1_w_in[l], ffn1_w_out[l])
        x = x + hybrid_mixer(rmsnorm(x, mix_norm[l]), mix_w_in[l], sb_q_gain[l], sb_k_gain[l],
                             conv_w[l], conv_b[l], conv_ln_g[l], conv_ln_b[l],
                             gla_gate_w[l], gla_gate_b[l], gla_out_gain[l], mix_w_out[l])
        x = x + 0.5 * swiglu_ffn(rmsnorm(x, ffn2_norm[l]), ffn2_w_in[l], ffn2_w_out[l])
        x = rmsnorm(x, final_norm[l])
    return x
```

```python
import numpy as np
from contextlib import ExitStack
import concourse.bass as bass
import concourse.mybir as mybir
from concourse.bass_utils import run_bass_kernel_spmd

F32 = mybir.dt.float32
BF16 = mybir.dt.bfloat16
AF = mybir.ActivationFunctionType
ALU = mybir.AluOpType

EPS = 1e-6
NCORES = 8
TT = 512
SB_HEADS = 12
GLA_HEADS = 6
CONV_K = 31


class Cfg:
    def __init__(self, D=4096, F=11008, S=8192, B=2, NTOK=4):
        self.D, self.F, self.S, self.B, self.NTOK = D, F, S, B, NTOK
        self.KC = D // 128
        self.FC = F // 128
        self.NT = S * B // NTOK
        self.NTT = self.NT // TT


FULL = Cfg()

ENGS = ("pe", "act", "dve", "pool", "sp")
EPOCH = 12000


class Buf:
    __slots__ = ("name", "w", "r")

    def __init__(self, name=""):
        self.name = name
        self.w = None
        self.r = []


class Op:
    __slots__ = ("eng", "fn", "waits", "idx", "slot", "val", "ms", "rank")


class Prog:
    def __init__(self, nc, es):
        self.nc, self.es = nc, es
        self.eobj = {"pe": nc.tensor, "act": nc.scalar, "dve": nc.vector, "pool": nc.gpsimd, "sp": nc.sync}
        self.ops = []
        self.cnt = {e: 0 for e in ENGS}
        self.seen = {e: {} for e in ENGS}
        self.slots = {}

    def _record(self, eng, fn, r, w, slot):
        o = Op()
        o.eng, o.fn, o.slot, o.ms, o.rank, o.val = eng, fn, slot, False, None, 0
        self.cnt[eng] += 1
        o.idx = self.cnt[eng]
        deps = []
        for b in r:
            if b.w is not None:
                deps.append(b.w)
        for b in w:
            if b.w is not None:
                deps.append(b.w)
            deps.extend(b.r)
        best = {}
        for d in deps:
            if d.slot is not None:
                key = ("dma", d.slot)
                v = d.val
            else:
                if d.eng == "pe" and eng == "pe" and slot is None:
                    continue
                key = ("eng", d.eng)
                v = d.idx
            if key not in best or best[key][0] < v:
                best[key] = (v, d)
        waits = []
        seen = self.seen[eng]
        for key, (v, d) in best.items():
            if seen.get(key, 0) >= v:
                continue
            seen[key] = v
            if key[0] == "eng":
                d.ms = True
            waits.append((key[0], d))
        o.waits = waits
        if slot is not None:
            st = self.slots.setdefault(slot, [None, 0])
            st[1] += 16
            o.val = st[1]
        for b in r:
            b.r.append(o)
        for b in w:
            b.w = o
            b.r = []
        self.ops.append(o)
        return o

    def op(self, eng, fn, r=(), w=()):
        return self._record(eng, fn, r, w, None)

    def dma(self, q, fn, r=(), w=(), slot=None):
        assert slot is not None
        return self._record(q, fn, r, w, slot)

    def emit(self, final_bufs=()):
        nc, es = self.nc, self.es
        self.op("sp", None, r=list(final_bufs))
        rank = {e: 0 for e in ENGS}
        sems = {e: [] for e in ENGS}
        for o in self.ops:
            if o.slot is None and o.ms:
                rank[o.eng] += 1
                o.rank = rank[o.eng]
        for e in ENGS:
            for i in range(rank[e] // EPOCH + 1):
                sems[e].append(es.enter_context(nc.semaphore("sem_%s_%d" % (e, i))))
        for k, st in self.slots.items():
            st[0] = es.enter_context(nc.semaphore("dsem_%s" % "_".join(str(x) for x in k)))
        for o in self.ops:
            E = self.eobj[o.eng]
            for kind, d in o.waits:
                if kind == "dma":
                    E.wait_ge(self.slots[d.slot][0], d.val)
                else:
                    ep, rk = (d.rank - 1) // EPOCH, (d.rank - 1) % EPOCH + 1
                    E.wait_ge(sems[d.eng][ep], rk)
            if o.fn is None:
                continue
            ins = o.fn(E)
            if o.slot is not None:
                ins.then_inc(self.slots[o.slot][0], 16)
            elif o.ms:
                ep = (o.rank - 1) // EPOCH
                ins.then_inc(sems[o.eng][ep], 1)


class Ring:
    def __init__(self, P, es, nc, name, n, shape, dtype):
        self.tiles = [es.enter_context(nc.sbuf_tensor("%s%d" % (name, i), shape, dtype)) for i in range(n)]
        self.bufs = [Buf("%s%d" % (name, i)) for i in range(n)]
        self.name, self.n, self.i = name, n, 0

    def next(self):
        k = self.i % self.n
        self.i += 1
        return self.tiles[k], self.bufs[k], (self.name, k)


class TokRes:
    def __init__(self, nc, es, cfg, P):
        KC, FC = cfg.KC, cfg.FC
        self.nc, self.cfg, self.P = nc, cfg, P
        self.hT = es.enter_context(nc.sbuf_tensor("hT", [128, KC, TT], BF16))
        self.hT_b = [Buf("hT%d" % k) for k in range(KC)]
        self.aT = es.enter_context(nc.sbuf_tensor("aT", [128, FC, TT], BF16))
        self.aT_b = [Buf("aT%d" % k) for k in range(FC)]
        self.wring = Ring(P, es, nc, "wr", 3, [128, 8192], BF16)
        self.xs = Ring(P, es, nc, "xs", 3, [128, TT], F32)
        self.sq = Ring(P, es, nc, "sq", 2, [128, TT], BF16)
        self.tmp = Ring(P, es, nc, "tmp", 3, [128, TT], F32)
        self.stf = Ring(P, es, nc, "stf", 3, [128, TT], F32)
        self.stb = Ring(P, es, nc, "stb", 3, [128, TT], BF16)
        self.rstd = es.enter_context(nc.sbuf_tensor("rstd", [128, TT], F32))
        self.rstd_b = Buf("rstd")
        self.ones = es.enter_context(nc.sbuf_tensor("ones", [128, 128], BF16))
        self.ones_b = Buf("ones")
        self.pb = [es.enter_context(nc.psum_tensor("pb%d" % i, [128, TT], F32)) for i in range(8)]
        self.pb_b = [Buf("pb%d" % i) for i in range(8)]
        ones = self.ones
        self.invD = es.enter_context(nc.sbuf_tensor("invD", [128, 128], BF16))
        self.inv128 = es.enter_context(nc.sbuf_tensor("inv128", [128, 128], BF16))
        invD, inv128 = self.invD, self.inv128
        P.op("dve", lambda e: e.memset(ones[:], 1.0), w=[self.ones_b])
        P.op("dve", lambda e: e.memset(invD[:], 1.0 / cfg.D), w=[self.ones_b])
        P.op("dve", lambda e: e.memset(inv128[:], 1.0 / 128.0), w=[self.ones_b])
        self.const_slot = 0
        self._bias = {}
        self._es = es

    def bias_tile(self, v):
        if v not in self._bias:
            t = self._es.enter_context(self.nc.sbuf_tensor("bias%d" % len(self._bias), [128, 1], F32))
            self.P.op("dve", lambda e: e.memset(t[:], float(v)), w=[self.ones_b])
            self._bias[v] = t
        return self._bias[v]

    def load_const(self, es, name, dram_ap, shape, dtype):
        nc, P = self.nc, self.P
        t = es.enter_context(nc.sbuf_tensor(name, shape, dtype))
        b = Buf(name)
        self.const_slot += 1
        q = "pool"
        P.dma(q, lambda e: e.dma_start(out=t[:], in_=dram_ap), w=[b], slot=("c", self.const_slot))
        return t, b


def emit_rsqrt(R, out, out_b, ps, ps_b, bias, n=TT):
    P = R.P
    bt = R.bias_tile(bias)
    P.op("act", lambda e: e.activation(out=out[:, 0:n], in_=ps[:, 0:n], func=AF.Sqrt, bias=bt[:, 0:1]), r=[ps_b, R.ones_b], w=[out_b])
    P.op("dve", lambda e: e.reciprocal(out=out[:, 0:n], in_=out[:, 0:n]), r=[out_b], w=[out_b])


def emit_norm(R, src, gain, gain_b, t0, ssq_bank=4, dst=None):
    P, cfg = R.P, R.cfg
    KC, D = cfg.KC, cfg.D
    ps, ps_b = R.pb[ssq_bank], R.pb_b[ssq_bank]
    for kc in range(KC):
        xt, xb, xk = R.xs.next()
        P.dma("sp", lambda e, xt=xt, kc=kc: e.dma_start(out=xt[:], in_=src[0][kc * 128:(kc + 1) * 128, t0:t0 + TT]),
              r=[src[1][kc]], w=[xb], slot=xk)
        st, sb, _ = R.sq.next()
        P.op("act", lambda e, xt=xt, st=st: e.activation(out=st[:], in_=xt[:], func=AF.Square), r=[xb], w=[sb])
        P.op("pe", lambda e, st=st, kc=kc: e.matmul(ps[:], R.invD[:], st[:], start=(kc == 0), stop=(kc == KC - 1)),
             r=[sb, R.ones_b], w=[ps_b])
    rstd = R.rstd
    emit_rsqrt(R, rstd, R.rstd_b, ps, ps_b, EPS)
    for kc in range(KC):
        xt, xb, xk = R.xs.next()
        P.dma("sp", lambda e, xt=xt, kc=kc: e.dma_start(out=xt[:], in_=src[0][kc * 128:(kc + 1) * 128, t0:t0 + TT]),
              r=[src[1][kc]], w=[xb], slot=xk)
        if dst is None:
            P.op("dve", lambda e, xt=xt, kc=kc: e.scalar_tensor_tensor(
                out=R.hT[:, kc, :], in0=xt[:], scalar=gain[:, kc:kc + 1], in1=rstd[:], op0=ALU.mult, op1=ALU.mult),
                r=[xb, R.rstd_b, gain_b], w=[R.hT_b[kc]])
        else:
            ot, ob, ok = R.stf.next()
            P.op("dve", lambda e, xt=xt, kc=kc, ot=ot: e.scalar_tensor_tensor(
                out=ot[:], in0=xt[:], scalar=gain[:, kc:kc + 1], in1=rstd[:], op0=ALU.mult, op1=ALU.mult),
                r=[xb, R.rstd_b, gain_b], w=[ob])
            P.dma("sp", lambda e, ot=ot, kc=kc: e.dma_start(out=dst[0][kc * 128:(kc + 1) * 128, t0:t0 + TT], in_=ot[:]),
                  r=[ob], w=[dst[1][kc]], slot=ok)


def emit_ffn(R, src, dst, gain, gain_b, w1, w2, t0):
    P, cfg = R.P, R.cfg
    KC, FC = cfg.KC, cfg.FC
    emit_norm(R, src, gain, gain_b, t0)
    for j in range(FC):
        wt, wb, wk = R.wring.next()
        P.dma("pool", lambda e, wt=wt, j=j: e.dma_start(out=wt[:, :2 * KC * 128], in_=w1[j]), w=[wb], slot=wk)
        banks = (2 * (j % 2), 2 * (j % 2) + 1)
        for g in range(2):
            bk = banks[g]
            for kc in range(KC):
                P.op("pe", lambda e, wt=wt, g=g, kc=kc, bk=bk: e.matmul(
                    R.pb[bk][:], wt[:, (g * KC + kc) * 128:(g * KC + kc + 1) * 128], R.hT[:, kc, :],
                    start=(kc == 0), stop=(kc == KC - 1)), r=[wb, R.hT_b[kc]], w=[R.pb_b[bk]])
        tt_, tb, _ = R.tmp.next()
        P.op("act", lambda e, tt_=tt_, bk=banks[0]: e.activation(out=tt_[:], in_=R.pb[bk][:], func=AF.Silu),
             r=[R.pb_b[banks[0]]], w=[tb])
        P.op("dve", lambda e, tt_=tt_, bk=banks[1], j=j: e.tensor_tensor(
            out=R.aT[:, j, :], in0=tt_[:], in1=R.pb[bk][:], op=ALU.mult),
            r=[tb, R.pb_b[banks[1]]], w=[R.aT_b[j]])
    h1 = (FC + 1) // 2
    for m in range(KC):
        parts = []
        for (a, b) in ((0, h1), (h1, FC)):
            wt, wb, wk = R.wring.next()
            P.dma("pool", lambda e, wt=wt, m=m, a=a, b=b: e.dma_start(
                out=wt[:, :(b - a) * 128], in_=w2[m][:, a * 128:b * 128]), w=[wb], slot=wk)
            parts.append((wt, wb, a, b))
        bk = m % 4
        for (wt, wb, a, b) in parts:
            for fc in range(a, b):
                P.op("pe", lambda e, wt=wt, fc=fc, a=a, bk=bk: e.matmul(
                    R.pb[bk][:], wt[:, (fc - a) * 128:(fc - a + 1) * 128], R.aT[:, fc, :],
                    start=(fc == 0), stop=(fc == FC - 1)), r=[wb, R.aT_b[fc]], w=[R.pb_b[bk]])
        xt, xb, xk = R.xs.next()
        P.dma("sp", lambda e, xt=xt, m=m: e.dma_start(out=xt[:], in_=src[0][m * 128:(m + 1) * 128, t0:t0 + TT]),
              r=[src[1][m]], w=[xb], slot=xk)
        ot, ob, ok = R.stf.next()
        P.op("dve", lambda e, ot=ot, xt=xt, bk=bk: e.scalar_tensor_tensor(
            out=ot[:], in0=R.pb[bk][:], scalar=0.5, in1=xt[:], op0=ALU.mult, op1=ALU.add),
            r=[R.pb_b[bk], xb], w=[ob])
        P.dma("sp", lambda e, ot=ot, m=m: e.dma_start(out=dst[0][m * 128:(m + 1) * 128, t0:t0 + TT], in_=ot[:]),
              r=[ob], w=[dst[1][m]], slot=ok)


def dram_in(nc, name, shape, dtype=F32):
    return nc.dram_tensor(name, list(shape), dtype, kind="ExternalInput").ap()


def dram_out(nc, name, shape, dtype=F32):
    return nc.dram_tensor(name, list(shape), dtype, kind="ExternalOutput").ap()


N_PAIRS = 32


def host_pairs():
    prs = []
    for h in range(12):
        prs.append((h, 12 + h, "sb"))
    for i in range(8):
        prs.append((36 + i, 44 + i, "conv"))
    for h in range(6):
        prs.append((52 + h, 58 + h, "gqk"))
    for i in range(6):
        prs.append((76 + 2 * i, 77 + 2 * i, "gr"))
    return prs


def build_A(cfg):
    nc = bass.Bass("TRN2", target_bir_lowering=False)
    D, KC, FC, NT = cfg.D, cfg.KC, cfg.FC, cfg.NT
    xT = dram_in(nc, "xT", [D, NT])
    n1 = dram_in(nc, "n1", [128, KC])
    w1 = dram_in(nc, "w1", [FC, 128, 2 * KC * 128])
    w2 = dram_in(nc, "w2", [KC, 128, FC * 128])
    nm = dram_in(nc, "nm", [128, KC])
    wmp = dram_in(nc, "wmp", [N_PAIRS, 128, 2 * KC * 128])
    wlr = dram_in(nc, "wlr", [128, KC * 16])
    wv = dram_in(nc, "wv", [12, 128, KC * 256])
    qg = dram_in(nc, "qg", [128, 1])
    kg = dram_in(nc, "kg", [128, 1])
    gw = dram_in(nc, "gw", [16, 768])
    gb = dram_in(nc, "gb", [1, 768])
    x1T = dram_out(nc, "x1T", [D, NT])
    qsT = dram_out(nc, "qsT", [1536, NT], BF16)
    ksT = dram_out(nc, "ksT", [1536, NT], BF16)
    vs = dram_out(nc, "vs", [NT, 1536], BF16)
    uT = dram_out(nc, "uT", [1024, NT])
    gqT = dram_out(nc, "gqT", [768, NT])
    gkT = dram_out(nc, "gkT", [768, NT])
    grT = dram_out(nc, "grT", [1536, NT])
    gv = dram_out(nc, "gv", [NT, 1536], BF16)
    gl = dram_out(nc, "gl", [NT, 768])
    with ExitStack() as es:
        P = Prog(nc, es)
        R = TokRes(nc, es, cfg, P)
        n1_t, n1_b = R.load_const(es, "n1s", n1, [128, KC], F32)
        nm_t, nm_b = R.load_const(es, "nms", nm, [128, KC], F32)
        qg_t, qg_b = R.load_const(es, "qgs", qg, [128, 1], F32)
        kg_t, kg_b = R.load_const(es, "kgs", kg, [128, 1], F32)
        gw_t, gw_b = R.load_const(es, "gws", gw, [16, 768], BF16)
        gb_t, gb_b = R.load_const(es, "gbs", gb, [1, 768], BF16)
        lrT = es.enter_context(nc.sbuf_tensor("lrT", [16, TT], BF16))
        lrT_b = Buf("lrT")
        x_src = (xT, [Buf() for _ in range(KC)])
        outs = []

        def mkout(ap, n):
            bl = [Buf() for _ in range(n)]
            outs.extend(bl)
            return (ap, bl)

        def inproj(t0, x1_dst):
            emit_norm(R, x1_dst, nm_t, nm_b, t0)
            for pi, (ca, cb, kind) in enumerate(host_pairs()):
                wt, wb, wk = R.wring.next()
                P.dma("pool", lambda e, wt=wt, pi=pi: e.dma_start(out=wt[:, :2 * KC * 128], in_=wmp[pi]),
                      w=[wb], slot=wk)
                banks = (2 * (pi % 2), 2 * (pi % 2) + 1)
                for g in range(2):
                    bk = banks[g]
                    for kc in range(KC):
                        P.op("pe", lambda e, wt=wt, g=g, kc=kc, bk=bk: e.matmul(
                            R.pb[bk][:], wt[:, (g * KC + kc) * 128:(g * KC + kc + 1) * 128], R.hT[:, kc, :],
                            start=(kc == 0), stop=(kc == KC - 1)), r=[wb, R.hT_b[kc]], w=[R.pb_b[bk]])
                if kind == "sb":
                    h = ca
                    for g, (gt, gbuf, dst_ap) in enumerate(((qg_t, qg_b, qsT), (kg_t, kg_b, ksT))):
                        bk = banks[g]
                        st, sb_, _ = R.sq.next()
                        P.op("act", lambda e, st=st, bk=bk: e.activation(out=st[:], in_=R.pb[bk][:], func=AF.Square),
                             r=[R.pb_b[bk]], w=[sb_])
                        nb = 5 + g
                        cst = R.ones if g == 0 else R.inv128
                        P.op("pe", lambda e, st=st, nb=nb, cst=cst: e.matmul(R.pb[nb][:], cst[:], st[:], start=True, stop=True),
                             r=[sb_, R.ones_b], w=[R.pb_b[nb]])
                        tt_, tb, _ = R.tmp.next()
                        emit_rsqrt(R, tt_, tb, R.pb[nb], R.pb_b[nb], 128.0 * EPS if g == 0 else EPS)
                        ot, ob, ok = R.stb.next()
                        P.op("dve", lambda e, ot=ot, tt_=tt_, bk=bk, gt=gt: e.scalar_tensor_tensor(
                            out=ot[:], in0=R.pb[bk][:], scalar=gt[:, 0:1], in1=tt_[:], op0=ALU.mult, op1=ALU.mult),
                            r=[R.pb_b[bk], tb, gbuf], w=[ob])
                        ob_d = Buf()
                        outs.append(ob_d)
                        P.dma("sp", lambda e, ot=ot, dst_ap=dst_ap, h=h: e.dma_start(
                            out=dst_ap[h * 128:(h + 1) * 128, t0:t0 + TT], in_=ot[:]), r=[ob], w=[ob_d], slot=ok)
                elif kind == "conv":
                    i = ca - 36
                    tt_, tb, _ = R.tmp.next()
                    P.op("act", lambda e, tt_=tt_, bk=banks[1]: e.activation(out=tt_[:], in_=R.pb[bk][:], func=AF.Sigmoid),
                         r=[R.pb_b[banks[1]]], w=[tb])
                    ot, ob, ok = R.stf.next()
                    P.op("dve", lambda e, ot=ot, tt_=tt_, bk=banks[0]: e.tensor_tensor(
                        out=ot[:], in0=tt_[:], in1=R.pb[bk][:], op=ALU.mult), r=[tb, R.pb_b[banks[0]]], w=[ob])
                    ob_d = Buf()
                    outs.append(ob_d)
                    P.dma("sp", lambda e, ot=ot, i=i: e.dma_start(out=uT[i * 128:(i + 1) * 128, t0:t0 + TT], in_=ot[:]),
                          r=[ob], w=[ob_d], slot=ok)
                else:
                    if kind == "gqk":
                        h = ca - 52
                        dsts = ((gqT, h), (gkT, h))
                    else:
                        dsts = ((grT, ca - 76), (grT, cb - 76))
                    for g, (dst_ap, row) in enumerate(dsts):
                        bk = banks[g]
                        ot, ob, ok = R.stf.next()
                        if g == 0:
                            P.op("act", lambda e, ot=ot, bk=bk: e.activation(out=ot[:], in_=R.pb[bk][:], func=AF.Copy),
                                 r=[R.pb_b[bk]], w=[ob])
                        else:
                            P.op("dve", lambda e, ot=ot, bk=bk: e.tensor_copy(out=ot[:], in_=R.pb[bk][:]),
                                 r=[R.pb_b[bk]], w=[ob])
                        ob_d = Buf()
                        outs.append(ob_d)
                        P.dma("sp", lambda e, ot=ot, dst_ap=dst_ap, row=row: e.dma_start(
                            out=dst_ap[row * 128:(row + 1) * 128, t0:t0 + TT], in_=ot[:]), r=[ob], w=[ob_d], slot=ok)
            wt, wb, wk = R.wring.next()
            P.dma("pool", lambda e, wt=wt: e.dma_start(out=wt[:, :KC * 16], in_=wlr), w=[wb], slot=wk)
            for kc in range(KC):
                P.op("pe", lambda e, wt=wt, kc=kc: e.matmul(
                    R.pb[7][0:16, :], wt[:, kc * 16:(kc + 1) * 16], R.hT[:, kc, :],
                    start=(kc == 0), stop=(kc == KC - 1)), r=[wb, R.hT_b[kc]], w=[R.pb_b[7]])
            P.op("act", lambda e: e.activation(out=lrT[:], in_=R.pb[7][0:16, :], func=AF.Copy),
                 r=[R.pb_b[7]], w=[lrT_b])
            for sub in range(TT // 128):
                for half in range(2):
                    bk = 5 + half
                    cs = slice(half * 384, (half + 1) * 384)
                    P.op("pe", lambda e, sub=sub, cs=cs, bk=bk: e.matmul(
                        R.pb[bk][:, 0:384], lrT[:, sub * 128:(sub + 1) * 128], gw_t[:, cs], start=True, stop=False),
                        r=[lrT_b, gw_b], w=[R.pb_b[bk]])
                    P.op("pe", lambda e, cs=cs, bk=bk: e.matmul(
                        R.pb[bk][:, 0:384], R.ones[0:1, :], gb_t[:, cs], start=False, stop=True),
                        r=[R.ones_b, gb_b], w=[R.pb_b[bk]])
                    tt_, tb, _ = R.tmp.next()
                    P.op("act", lambda e, tt_=tt_, bk=bk: e.activation(
                        out=tt_[:, 0:384], in_=R.pb[bk][:, 0:384], func=AF.Exp, scale=-1.0), r=[R.pb_b[bk]], w=[tb])
                    ot, ob, ok = R.stf.next()
                    b1 = R.bias_tile(1.0)
                    P.op("act", lambda e, tt_=tt_, ot=ot, b1=b1: e.activation(
                        out=ot[:, 0:384], in_=tt_[:, 0:384], func=AF.Ln, bias=b1[:, 0:1]), r=[tb, R.ones_b], w=[ob])
                    ob_d = Buf()
                    outs.append(ob_d)
                    P.dma("sp", lambda e, ot=ot, sub=sub, cs=cs: e.dma_start(
                        out=gl[t0 + sub * 128:t0 + (sub + 1) * 128, cs], in_=ot[:, 0:384]), r=[ob], w=[ob_d], slot=ok)
            for blk in range(12):
                wt, wb, wk = R.wring.next()
                P.dma("pool", lambda e, wt=wt, blk=blk: e.dma_start(out=wt[:, :KC * 256], in_=wv[blk]), w=[wb], slot=wk)
                dst_ap, c0 = (vs, blk * 256) if blk < 6 else (gv, (blk - 6) * 256)
                for sub in range(TT // 128):
                    bk = (blk * 4 + sub) % 4
                    for kc in range(KC):
                        P.op("pe", lambda e, wt=wt, kc=kc, sub=sub, bk=bk: e.matmul(
                            R.pb[bk][:, 0:256], R.hT[:, kc, sub * 128:(sub + 1) * 128], wt[:, kc * 256:(kc + 1) * 256],
                            start=(kc == 0), stop=(kc == KC - 1)), r=[wb, R.hT_b[kc]], w=[R.pb_b[bk]])
                    ot, ob, ok = R.stb.next()
                    if sub % 2 == 0:
                        P.op("act", lambda e, ot=ot, bk=bk: e.activation(out=ot[:, 0:256], in_=R.pb[bk][:, 0:256], func=AF.Copy),
                             r=[R.pb_b[bk]], w=[ob])
                    else:
                        P.op("dve", lambda e, ot=ot, bk=bk: e.tensor_copy(out=ot[:, 0:256], in_=R.pb[bk][:, 0:256]),
                             r=[R.pb_b[bk]], w=[ob])
                    ob_d = Buf()
                    outs.append(ob_d)
                    P.dma("sp", lambda e, ot=ot, dst_ap=dst_ap, c0=c0, sub=sub: e.dma_start(
                        out=dst_ap[t0 + sub * 128:t0 + (sub + 1) * 128, c0:c0 + 256], in_=ot[:, 0:256]),
                        r=[ob], w=[ob_d], slot=ok)

        for t in range(cfg.NTT):
            t0 = t * TT
            x1_dst = mkout(x1T, KC)
            emit_ffn(R, x_src, x1_dst, n1_t, n1_b, w1, w2, t0)
            inproj(t0, x1_dst)
        P.emit(final_bufs=outs)
    return nc


def lay_gain(g):
    return np.ascontiguousarray(g.reshape(-1, 128).T)


def lay_w1(w_in, F):
    D = w_in.shape[0]
    KC, FC = D // 128, F // 128
    a = w_in.reshape(KC, 128, 2, FC, 128)
    return np.ascontiguousarray(a.transpose(3, 1, 2, 0, 4)).reshape(FC, 128, 2 * KC * 128)


def lay_w2(w_out):
    F, D = w_out.shape
    KC, FC = D // 128, F // 128
    a = w_out.reshape(FC, 128, KC, 128)
    return np.ascontiguousarray(a.transpose(2, 1, 0, 3)).reshape(KC, 128, FC * 128)


def lay_cols(w, cols, width):
    D = w.shape[0]
    KC = D // 128
    a = w[:, cols:cols + width].reshape(KC, 128, width)
    return np.ascontiguousarray(a.transpose(1, 0, 2)).reshape(128, KC * width)


def lay_wmp(wm):
    D = wm.shape[0]
    KC = D // 128
    out = np.empty((N_PAIRS, 128, 2 * KC * 128), np.float32)
    for pi, (ca, cb, _) in enumerate(host_pairs()):
        out[pi, :, :KC * 128] = lay_cols(wm, ca * 128, 128)
        out[pi, :, KC * 128:] = lay_cols(wm, cb * 128, 128)
    return out


def lay_wv(wm):
    D = wm.shape[0]
    KC = D // 128
    out = np.empty((12, 128, KC * 256), np.float32)
    for b in range(6):
        out[b] = lay_cols(wm, 24 * 128 + b * 256, 256)
        out[6 + b] = lay_cols(wm, 64 * 128 + b * 256, 256)
    return out


def tok_shard_T(x, cfg):
    B, S, D = x.shape
    res = []
    for c in range(cfg.NTOK):
        b, q = divmod(c, cfg.NTOK // B)
        res.append(np.ascontiguousarray(x[b, q * cfg.NT:(q + 1) * cfg.NT, :].T))
    return res


_NC_CACHE = {}


def _get_nc(key, builder):
    if key not in _NC_CACHE:
        _NC_CACHE[key] = builder()
    return _NC_CACHE[key]


def run_A(cfg, xT_list, l, inp):
    nc = _get_nc(("A", cfg.F, cfg.S, cfg.NTOK), lambda: build_A(cfg))
    shared = {
        "n1": lay_gain(inp["ffn1_norm"][l]), "w1": lay_w1(inp["ffn1_w_in"][l], cfg.F), "w2": lay_w2(inp["ffn1_w_out"][l]),
        "nm": lay_gain(inp["mix_norm"][l]), "wmp": lay_wmp(inp["mix_w_in"][l]),
        "wlr": lay_cols(inp["mix_w_in"][l], 88 * 128, 16), "wv": lay_wv(inp["mix_w_in"][l]),
        "qg": np.ascontiguousarray(inp["sb_q_gain"][l].reshape(128, 1)),
        "kg": np.ascontiguousarray(inp["sb_k_gain"][l].reshape(128, 1)),
        "gw": np.ascontiguousarray(inp["gla_gate_w"][l]), "gb": np.ascontiguousarray(inp["gla_gate_b"][l].reshape(1, 768)),
    }
    in_maps = [dict(shared, xT=xT_list[c]) for c in range(cfg.NTOK)]
    res = run_bass_kernel_spmd(nc, in_maps, core_ids=list(range(cfg.NTOK)))
    return res.results


QT = 512
GSEG = 1024
CSEG = 2048
GLA_SCALE = 128.0 ** -0.5


def b_consts():
    j = np.arange(128)[:, None]
    s = np.arange(128)[None, :]
    masks = np.zeros((128, 4, QT), np.float32)
    t = np.arange(QT)[None, :]
    for d in range(4):
        masks[:, d, :] = ((128 * d + j) < t).astype(np.float32)
    return {
        "masks": masks,
        "tge": (j >= s).astype(np.float32),
        "tlt": (j < s).astype(np.float32),
        "uinc": ((j <= s) * (-1.0 / 16.0)).astype(np.float32),
        "masku": (j <= s).astype(np.float32),
        "ident": np.eye(128, dtype=np.float32),
    }


def build_B(S):
    nc = bass.Bass("TRN2", target_bir_lowering=False)
    NCH = S // 128
    NQ = S // QT
    qsT = dram_in(nc, "qsT", [3, 128, S], BF16)
    ksT = dram_in(nc, "ksT", [3, 128, S], BF16)
    vs = dram_in(nc, "vs", [3, 128, NCH * 128], BF16)
    uT = dram_in(nc, "uT", [2, 128, S])
    cw = dram_in(nc, "cw", [2, 128, CONV_K])
    cb = dram_in(nc, "cb", [2, 128, 1])
    gqT = dram_in(nc, "gqT", [2, 128, S])
    gkT = dram_in(nc, "gkT", [2, 128, S])
    gv = dram_in(nc, "gv", [2, 128, NCH * 256], BF16)
    gl = dram_in(nc, "gl", [2, 128, NCH * 128])
    grT = dram_in(nc, "grT", [2, 128, 2, S])
    gg = dram_in(nc, "gg", [128, 2])
    masks_d = dram_in(nc, "masks", [128, 4, QT])
    tge_d = dram_in(nc, "tge", [128, 128])
    tlt_d = dram_in(nc, "tlt", [128, 128])
    uinc_d = dram_in(nc, "uinc", [128, 128])
    masku_d = dram_in(nc, "masku", [128, 128])
    ident_d = dram_in(nc, "ident", [128, 128])
    yaT = dram_out(nc, "yaT", [3, 128, S], BF16)
    cyT = dram_out(nc, "cyT", [2, 128, S])
    ycT = dram_out(nc, "ycT", [2, 128, 2, S], BF16)
    with ExitStack() as es:
        P = Prog(nc, es)
        outs = []
        cslot = [0]

        def const(name, ap, shape, dtype):
            t = es.enter_context(nc.sbuf_tensor(name, shape, dtype))
            b = Buf(name)
            cslot[0] += 1
            P.dma("pool", lambda e: e.dma_start(out=t[:], in_=ap), w=[b], slot=("c", cslot[0]))
            return t, b

        masks, masks_b = const("masks_s", masks_d, [128, 4, QT], F32)
        tge, tge_b = const("tge_s", tge_d, [128, 128], F32)
        tlt, tlt_b = const("tlt_s", tlt_d, [128, 128], F32)
        uinc, uinc_b = const("uinc_s", uinc_d, [128, 128], F32)
        masku, masku_b = const("masku_s", masku_d, [128, 128], F32)
        ident, ident_b = const("ident_s", ident_d, [128, 128], BF16)
        gg_t, gg_b = const("gg_s", gg, [128, 2], F32)
        cst_b = Buf("cst")
        bias1 = es.enter_context(nc.sbuf_tensor("bias1", [128, 1], F32))
        biase = es.enter_context(nc.sbuf_tensor("biase", [128, 1], F32))
        inv256 = es.enter_context(nc.sbuf_tensor("inv256", [128, 128], F32))
        P.op("dve", lambda e: e.memset(bias1[:], 1.0), w=[cst_b])
        P.op("dve", lambda e: e.memset(biase[:], EPS), w=[cst_b])
        P.op("dve", lambda e: e.memset(inv256[:], 1.0 / 256.0), w=[cst_b])
        pb = [es.enter_context(nc.psum_tensor("pb%d" % i, [128, 512], F32)) for i in range(7)]
        pb_b = [Buf("pb%d" % i) for i in range(7)]
        pT = es.enter_context(nc.psum_tensor("pT", [128, 128], BF16))
        pT_b = Buf("pT")

        qT = es.enter_context(nc.sbuf_tensor("qT", [128, S], BF16))
        kT = es.enter_context(nc.sbuf_tensor("kT", [128, S], BF16))
        V = es.enter_context(nc.sbuf_tensor("V", [128, NCH, 128], BF16))
        q_b, k_b, v_b = Buf("q"), Buf("k"), Buf("v")
        Er = Ring(P, es, nc, "E", 3, [128, QT], F32)
        Lr = Ring(P, es, nc, "L", 3, [128, QT], F32)
        Gr = Ring(P, es, nc, "G", 3, [128, QT], F32)
        Ar = Ring(P, es, nc, "A", 3, [128, QT], BF16)
        Or = Ring(P, es, nc, "O", 2, [128, QT], BF16)

        def sb_tile_steps(hh, qi, stream):
            t0 = qi * QT
            cmax = 4 * qi + 3
            pacc, pacc_b = pb[2 + stream], pb_b[2 + stream]
            po, po_b = pb[4 + stream], pb_b[4 + stream]
            for c in range(cmax, -1, -1):
                def step(c=c):
                    first, last = (c == cmax), (c == 0)
                    zb = stream
                    P.op("pe", lambda e: e.matmul(pb[zb][:], kT[:, c * 128:(c + 1) * 128], qT[:, t0:t0 + QT],
                                                  start=True, stop=True), r=[k_b, q_b], w=[pb_b[zb]])
                    et, eb, _ = Er.next()
                    P.op("act", lambda e: e.activation(out=et[:], in_=pb[zb][:], func=AF.Exp), r=[pb_b[zb]], w=[eb])
                    d = c - 4 * qi
                    if d >= 0:
                        P.op("dve", lambda e: e.tensor_tensor(out=et[:], in0=et[:], in1=masks[:, d, :], op=ALU.mult),
                             r=[eb, masks_b], w=[eb])
                    lt, lb, _ = Lr.next()
                    P.op("act", lambda e: e.activation(out=lt[:], in_=et[:], func=AF.Ln, bias=bias1[:, 0:1]),
                         r=[eb, cst_b], w=[lb])
                    P.op("pe", lambda e: e.matmul(pacc[:], tge[:], lt[:], start=first, stop=True, skip_group_check=True),
                         r=[tge_b, lb], w=[pacc_b])
                    gt, gb_, _ = Gr.next()
                    P.op("act", lambda e: e.activation(out=gt[:], in_=pacc[:], func=AF.Exp, scale=-1.0),
                         r=[pacc_b], w=[gb_])
                    if not last:
                        P.op("pe", lambda e: e.matmul(pacc[:], tlt[:], lt[:], start=False, stop=True, skip_group_check=True),
                             r=[tlt_b, lb], w=[pacc_b])
                    at, ab, _ = Ar.next()
                    P.op("pool", lambda e: e.tensor_tensor(out=at[:], in0=et[:], in1=gt[:], op=ALU.mult),
                         r=[eb, gb_], w=[ab])
                    P.op("pe", lambda e: e.matmul(po[:], V[:, c, :], at[:], start=first, stop=last),
                         r=[v_b, ab], w=[po_b])
                    if last:
                        ot, ob, ok = Or.next()
                        P.op("dve", lambda e: e.tensor_copy(out=ot[:], in_=po[:]), r=[po_b], w=[ob])
                        od = Buf()
                        outs.append(od)
                        P.dma("sp", lambda e: e.dma_start(out=yaT[hh, :, t0:t0 + QT], in_=ot[:]), r=[ob], w=[od], slot=ok)
                yield step

        def sb_head(hh):
            P.dma("sp", lambda e: e.dma_start(out=qT[:], in_=qsT[hh]), w=[q_b], slot=("ld", 0))
            P.dma("sp", lambda e: e.dma_start(out=kT[:], in_=ksT[hh]), w=[k_b], slot=("ld", 1))
            P.dma("sp", lambda e: e.dma_start(out=V[:].rearrange("p c d -> p (c d)"), in_=vs[hh]), w=[v_b], slot=("ld", 2))
            loads = [0, 0]
            streams = [[], []]
            for qi in sorted(range(NQ), key=lambda q: -q):
                s_ = 0 if loads[0] <= loads[1] else 1
                streams[s_].append(qi)
                loads[s_] += 4 * qi + 4
            steps = [[st for qi in streams[s_] for st in sb_tile_steps(hh, qi, s_)] for s_ in range(2)]
            for i in range(max(len(steps[0]), len(steps[1]))):
                for s_ in range(2):
                    if i < len(steps[s_]):
                        steps[s_][i]()

        NCS = max(1, S // CSEG)
        cseg = min(CSEG, S)
        Ur = Ring(P, es, nc, "U", 2, [128, CONV_K - 1 + cseg], F32)
        Cr = Ring(P, es, nc, "C", 2, [128, cseg], F32)
        cw_t = es.enter_context(nc.sbuf_tensor("cw_s", [128, 2, CONV_K], F32))
        cb_t = es.enter_context(nc.sbuf_tensor("cb_s", [128, 2], F32))
        cwb = Buf("cw")
        for gi in range(2):
            P.dma("pool", lambda e, gi=gi: e.dma_start(out=cw_t[:, gi, :], in_=cw[gi]), w=[cwb], slot=("cw", gi))
            P.dma("pool", lambda e, gi=gi: e.dma_start(out=cb_t[:, gi:gi + 1], in_=cb[gi]), w=[cwb], slot=("cb", gi))

        def conv_seg(gi, si):
            ut, ub, uk = Ur.next()
            s0 = si * cseg
            PAD = CONV_K - 1
            if si == 0:
                P.op("dve", lambda e: e.memset(ut[:, 0:PAD], 0.0), w=[ub])
                P.dma("sp", lambda e: e.dma_start(out=ut[:, PAD:], in_=uT[gi, :, 0:cseg]), w=[ub], slot=uk)
            else:
                P.dma("sp", lambda e: e.dma_start(out=ut[:], in_=uT[gi, :, s0 - PAD:s0 + cseg]), w=[ub], slot=uk)
            ct, cbuf, ck = Cr.next()
            P.op("dve", lambda e: e.tensor_scalar(out=ct[:], in0=ut[:, 0:cseg], scalar1=cw_t[:, gi, 0:1],
                                                  scalar2=cb_t[:, gi:gi + 1], op0=ALU.mult, op1=ALU.add),
                 r=[ub, cwb], w=[cbuf])
            for k in range(1, CONV_K):
                P.op("dve", lambda e, k=k: e.scalar_tensor_tensor(
                    out=ct[:], in0=ut[:, k:k + cseg], scalar=cw_t[:, gi, k:k + 1], in1=ct[:], op0=ALU.mult, op1=ALU.add),
                    r=[ub, cwb, cbuf], w=[cbuf])
            od = Buf()
            outs.append(od)
            P.dma("sp", lambda e: e.dma_start(out=cyT[gi, :, s0:s0 + cseg], in_=ct[:]), r=[cbuf], w=[od], slot=ck)

        gseg = min(GSEG, S)
        NGS = S // gseg
        CPS = gseg // 128
        Qr = Ring(P, es, nc, "gq", 2, [128, gseg], F32)
        Kr = Ring(P, es, nc, "gk", 2, [128, gseg], F32)
        Vr = Ring(P, es, nc, "gvv", 2, [128, CPS, 256], BF16)
        Lgr = Ring(P, es, nc, "gll", 2, [128, CPS, 128], F32)
        Rr = Ring(P, es, nc, "grr", 2, [128, 2, gseg], F32)
        Yr = Ring(P, es, nc, "gy", 2, [128, 2, gseg], BF16)
        state = es.enter_context(nc.sbuf_tensor("state", [128, 256], F32))
        state_bf = es.enter_context(nc.sbuf_tensor("state_bf", [128, 256], BF16))
        st_b, stbf_b = Buf("state"), Buf("state_bf")
        EBr = Ring(P, es, nc, "eb", 2, [128, 128], F32)
        ENr = Ring(P, es, nc, "en", 2, [128, 128], F32)
        QDr = Ring(P, es, nc, "qd", 2, [128, 128], BF16)
        KDr = Ring(P, es, nc, "kd", 2, [128, 128], BF16)
        KTr = Ring(P, es, nc, "kt", 2, [128, 128], BF16)
        SCr = Ring(P, es, nc, "sc", 2, [128, 128], BF16)
        SQr = Ring(P, es, nc, "sqo", 2, [128, 256], F32)
        RSr = Ring(P, es, nc, "rs", 2, [128, 128], F32)
        SRr = Ring(P, es, nc, "sr", 2, [128, 2, 128], F32)
        T1r = Ring(P, es, nc, "t1", 2, [128, 2, 128], F32)
        TSr = Ring(P, es, nc, "ts", 2, [128, 256], F32)

        def gla_head(sl):
            P.op("dve", lambda e: e.memset(state[:], 0.0), w=[st_b])
            P.op("dve", lambda e: e.memset(state_bf[:], 0.0), w=[stbf_b])
            for si in range(NGS):
                s0 = si * gseg
                qt, qb_, qk = Qr.next()
                kt, kb_, kk = Kr.next()
                vt, vb_, vk = Vr.next()
                lt, lb_, lk = Lgr.next()
                rt, rb_, rk = Rr.next()
                yt, yb_, yk = Yr.next()
                P.dma("sp", lambda e, qt=qt, s0=s0: e.dma_start(out=qt[:], in_=gqT[sl, :, s0:s0 + gseg]), w=[qb_], slot=qk)
                P.dma("sp", lambda e, kt=kt, s0=s0: e.dma_start(out=kt[:], in_=gkT[sl, :, s0:s0 + gseg]), w=[kb_], slot=kk)
                P.dma("sp", lambda e, vt=vt, si=si: e.dma_start(
                    out=vt[:].rearrange("p c d -> p (c d)"), in_=gv[sl, :, si * CPS * 256:(si + 1) * CPS * 256]), w=[vb_], slot=vk)
                P.dma("sp", lambda e, lt=lt, si=si: e.dma_start(
                    out=lt[:].rearrange("p c d -> p (c d)"), in_=gl[sl, :, si * CPS * 128:(si + 1) * CPS * 128]), w=[lb_], slot=lk)
                P.dma("sp", lambda e, rt=rt, s0=s0: e.dma_start(out=rt[:], in_=grT[sl, :, :, s0:s0 + gseg]), w=[rb_], slot=rk)
                for ci in range(CPS):
                    gla_chunk(sl, qt, qb_, kt, kb_, vt, vb_, lt, lb_, rt, rb_, yt, yb_, ci)
                od = Buf()
                outs.append(od)
                P.dma("sp", lambda e, yt=yt, s0=s0: e.dma_start(out=ycT[sl, :, :, s0:s0 + gseg], in_=yt[:]),
                      r=[yb_], w=[od], slot=yk)

        def gla_chunk(sl, qt, qb_, kt, kb_, vt, vb_, lt, lb_, rt, rb_, yt, yb_, ci):
            cs = slice(ci * 128, (ci + 1) * 128)
            pB, pSc, pO, pS, pN = pb[0], pb[1], pb[2], pb[3], pb[4]
            pB_b, pSc_b, pO_b, pS_b, pN_b = pb_b[0], pb_b[1], pb_b[2], pb_b[3], pb_b[4]
            P.op("pe", lambda e: e.matmul(pB[:, 0:128], lt[:, ci, :], uinc[:], start=True, stop=True),
                 r=[lb_, uinc_b], w=[pB_b])
            ebt, ebb, _ = EBr.next()
            ent, enb, _ = ENr.next()
            P.op("act", lambda e: e.activation(out=ebt[:], in_=pB[:, 0:128], func=AF.Exp), r=[pB_b], w=[ebb])
            P.op("act", lambda e: e.activation(out=ent[:], in_=pB[:, 0:128], func=AF.Exp, scale=-1.0), r=[pB_b], w=[enb])
            qd, qdb, _ = QDr.next()
            kd, kdb, _ = KDr.next()
            P.op("dve", lambda e: e.scalar_tensor_tensor(out=qd[:], in0=qt[:, cs], scalar=GLA_SCALE, in1=ebt[:],
                                                         op0=ALU.mult, op1=ALU.mult), r=[qb_, ebb], w=[qdb])
            P.op("dve", lambda e: e.tensor_tensor(out=kd[:], in0=kt[:, cs], in1=ent[:], op=ALU.mult), r=[kb_, enb], w=[kdb])
            P.op("pe", lambda e: e.transpose(pT[:], kd[:], ident[:]), r=[kdb, ident_b], w=[pT_b])
            ktt, ktb, _ = KTr.next()
            P.op("act", lambda e: e.activation(out=ktt[:], in_=pT[:], func=AF.Copy), r=[pT_b], w=[ktb])
            P.op("pe", lambda e: e.matmul(pSc[:, 0:128], kd[:], qd[:], start=True, stop=True), r=[kdb, qdb], w=[pSc_b])
            sc, scb, _ = SCr.next()
            P.op("dve", lambda e: e.tensor_tensor(out=sc[:], in0=pSc[:, 0:128], in1=masku[:], op=ALU.mult),
                 r=[pSc_b, masku_b], w=[scb])
            for half in range(2):
                hs = slice(half * 128, (half + 1) * 128)
                P.op("pe", lambda e, hs=hs: e.matmul(pO[:, hs], vt[:, ci, hs], sc[:], start=True, stop=False),
                     r=[vb_, scb], w=[pO_b])
                P.op("pe", lambda e, hs=hs: e.matmul(pO[:, hs], state_bf[:, hs], qd[:], start=False, stop=True),
                     r=[stbf_b, qdb], w=[pO_b])
            P.op("pe", lambda e: e.matmul(pS[:, 0:256], ktt[:], vt[:, ci, :], start=True, stop=True), r=[ktb, vb_], w=[pS_b])
            ts, tsb, _ = TSr.next()
            P.op("dve", lambda e: e.tensor_tensor(out=ts[:], in0=pS[:, 0:256], in1=state[:], op=ALU.add),
                 r=[pS_b, st_b], w=[tsb])
            P.op("dve", lambda e: e.tensor_scalar(out=state[:], in0=ts[:], scalar1=ebt[:, 127:128], scalar2=0.0, op0=ALU.mult, op1=ALU.add),
                 r=[tsb, ebb], w=[st_b])
            P.op("dve", lambda e: e.tensor_copy(out=state_bf[:], in_=state[:]), r=[st_b], w=[stbf_b])
            sq, sqb, _ = SQr.next()
            P.op("act", lambda e: e.activation(out=sq[:], in_=pO[:, 0:256], func=AF.Square), r=[pO_b], w=[sqb])
            P.op("pe", lambda e: e.matmul(pN[:, 0:128], inv256[:], sq[:, 0:128], start=True, stop=False), r=[sqb, cst_b], w=[pN_b])
            P.op("pe", lambda e: e.matmul(pN[:, 0:128], inv256[:], sq[:, 128:256], start=False, stop=True), r=[sqb, cst_b], w=[pN_b])
            rs, rsb, _ = RSr.next()
            P.op("act", lambda e: e.activation(out=rs[:], in_=pN[:, 0:128], func=AF.Sqrt, bias=biase[:, 0:1]),
                 r=[pN_b, cst_b], w=[rsb])
            P.op("dve", lambda e: e.reciprocal(out=rs[:], in_=rs[:]), r=[rsb], w=[rsb])
            sr, srb, _ = SRr.next()
            P.op("act", lambda e: e.activation(out=sr[:], in_=rt[:, :, cs], func=AF.Silu), r=[rb_], w=[srb])
            t1, t1b, _ = T1r.next()
            for half in range(2):
                hs = slice(half * 128, (half + 1) * 128)
                P.op("dve", lambda e, half=half, hs=hs: e.scalar_tensor_tensor(
                    out=t1[:, half, :], in0=pO[:, hs], scalar=gg_t[:, half:half + 1], in1=rs[:], op0=ALU.mult, op1=ALU.mult),
                    r=[pO_b, gg_b, rsb], w=[t1b])
            P.op("pool", lambda e: e.tensor_tensor(out=yt[:, :, cs], in0=t1[:], in1=sr[:], op=ALU.mult),
                 r=[t1b, srb], w=[yb_])

        conv_jobs = [(gi, si) for gi in range(2) for si in range(NCS)]
        for hh in range(3):
            per = (len(conv_jobs) + 2) // 3
            for (gi, si) in conv_jobs[hh * per:(hh + 1) * per]:
                conv_seg(gi, si)
            sb_head(hh)
        for sl in range(2):
            gla_head(sl)
        P.emit(final_bufs=outs)
    return nc


def lay_tokmajor(v, width):
    S = v.shape[0]
    return np.ascontiguousarray(v.reshape(S // 128, 128, width).transpose(1, 0, 2)).reshape(128, (S // 128) * width)


def build_C(cfg):
    nc = bass.Bass("TRN2", target_bir_lowering=False)
    D, KC, FC, NT = cfg.D, cfg.KC, cfg.FC, cfg.NT
    x1T = dram_in(nc, "x1T", [D, NT])
    yaT = dram_in(nc, "yaT", [1536, NT], BF16)
    cyT = dram_in(nc, "cyT", [1024, NT])
    ycT = dram_in(nc, "ycT", [1536, NT], BF16)
    lng = dram_in(nc, "lng", [128, 8])
    lnb = dram_in(nc, "lnb", [128, 8])
    wo = dram_in(nc, "wo", [KC, 128, KC * 128])
    n2 = dram_in(nc, "n2", [128, KC])
    w1 = dram_in(nc, "w1", [FC, 128, 2 * KC * 128])
    w2 = dram_in(nc, "w2", [KC, 128, FC * 128])
    nf = dram_in(nc, "nf", [128, KC])
    x2T = nc.dram_tensor("x2T", [D, NT], F32, kind="Internal").ap()
    x3T = nc.dram_tensor("x3T", [D, NT], F32, kind="Internal").ap()
    xoT = dram_out(nc, "xoT", [D, NT])
    with ExitStack() as es:
        P = Prog(nc, es)
        R = TokRes(nc, es, cfg, P)
        n2_t, n2_b = R.load_const(es, "n2s", n2, [128, KC], F32)
        nf_t, nf_b = R.load_const(es, "nfs", nf, [128, KC], F32)
        lng_t, lng_b = R.load_const(es, "lngs", lng, [128, 8], F32)
        lnb_t, lnb_b = R.load_const(es, "lnbs", lnb, [128, 8], F32)
        inv1024 = es.enter_context(nc.sbuf_tensor("inv1024", [128, 128], BF16))
        P.op("dve", lambda e: e.memset(inv1024[:], 1.0 / 1024.0), w=[R.ones_b])
        mean_t = es.enter_context(nc.sbuf_tensor("mean_t", [128, TT], F32))
        mean_b = Buf("mean")
        var_t = es.enter_context(nc.sbuf_tensor("var_t", [128, TT], F32))
        var_b = Buf("var")
        outs = []
        no_b = [Buf() for _ in range(KC)]

        def tile_C(t0):
            for c in range(12):
                P.dma("sp", lambda e, c=c: e.dma_start(out=R.hT[:, c, :], in_=yaT[c * 128:(c + 1) * 128, t0:t0 + TT]),
                      w=[R.hT_b[c]], slot=("y", c))
                P.dma("sp", lambda e, c=c: e.dma_start(out=R.hT[:, 20 + c, :], in_=ycT[c * 128:(c + 1) * 128, t0:t0 + TT]),
                      w=[R.hT_b[20 + c]], slot=("y", 20 + c))
            pm, pm_b, pe2, pe2_b = R.pb[5], R.pb_b[5], R.pb[6], R.pb_b[6]
            for i in range(8):
                xt, xb, xk = R.xs.next()
                P.dma("sp", lambda e, xt=xt, i=i: e.dma_start(out=xt[:], in_=cyT[i * 128:(i + 1) * 128, t0:t0 + TT]),
                      w=[xb], slot=xk)
                st, sb_, _ = R.sq.next()
                P.op("act", lambda e, xt=xt, st=st: e.activation(out=st[:], in_=xt[:], func=AF.Square), r=[xb], w=[sb_])
                P.op("pe", lambda e, st=st, i=i: e.matmul(pe2[:], inv1024[:], st[:], start=(i == 0), stop=(i == 7)),
                     r=[sb_, R.ones_b], w=[pe2_b])
                st2, sb2, _ = R.sq.next()
                P.op("dve", lambda e, xt=xt, st2=st2: e.tensor_copy(out=st2[:], in_=xt[:]), r=[xb], w=[sb2])
                P.op("pe", lambda e, st2=st2, i=i: e.matmul(pm[:], inv1024[:], st2[:], start=(i == 0), stop=(i == 7)),
                     r=[sb2, R.ones_b], w=[pm_b])
            P.op("act", lambda e: e.activation(out=mean_t[:], in_=pm[:], func=AF.Copy), r=[pm_b], w=[mean_b])
            P.op("dve", lambda e: e.tensor_tensor(out=var_t[:], in0=mean_t[:], in1=mean_t[:], op=ALU.mult), r=[mean_b], w=[var_b])
            P.op("dve", lambda e: e.tensor_tensor(out=var_t[:], in0=pe2[:], in1=var_t[:], op=ALU.subtract),
                 r=[pe2_b, var_b], w=[var_b])
            bt = R.bias_tile(EPS)
            P.op("act", lambda e: e.activation(out=var_t[:], in_=var_t[:], func=AF.Sqrt, bias=bt[:, 0:1]),
                 r=[var_b, R.ones_b], w=[var_b])
            P.op("dve", lambda e: e.reciprocal(out=var_t[:], in_=var_t[:]), r=[var_b], w=[var_b])
            for i in range(8):
                xt, xb, xk = R.xs.next()
                P.dma("sp", lambda e, xt=xt, i=i: e.dma_start(out=xt[:], in_=cyT[i * 128:(i + 1) * 128, t0:t0 + TT]),
                      w=[xb], slot=xk)
                P.op("dve", lambda e, xt=xt: e.tensor_tensor(out=xt[:], in0=xt[:], in1=mean_t[:], op=ALU.subtract),
                     r=[xb, mean_b], w=[xb])
                P.op("dve", lambda e, xt=xt: e.tensor_tensor(out=xt[:], in0=xt[:], in1=var_t[:], op=ALU.mult),
                     r=[xb, var_b], w=[xb])
                P.op("dve", lambda e, xt=xt, i=i: e.tensor_scalar(
                    out=xt[:], in0=xt[:], scalar1=lng_t[:, i:i + 1], scalar2=lnb_t[:, i:i + 1], op0=ALU.mult, op1=ALU.add),
                    r=[xb, lng_b, lnb_b], w=[xb])
                P.op("act", lambda e, xt=xt, i=i: e.activation(out=R.hT[:, 12 + i, :], in_=xt[:], func=AF.Silu),
                     r=[xb], w=[R.hT_b[12 + i]])
            x2_dst = (x2T, [Buf() for _ in range(KC)])
            for m in range(KC):
                wt, wb, wk = R.wring.next()
                P.dma("pool", lambda e, wt=wt, m=m: e.dma_start(out=wt[:, :KC * 128], in_=wo[m]), w=[wb], slot=wk)
                bk = m % 4
                for kc in range(KC):
                    P.op("pe", lambda e, wt=wt, kc=kc, bk=bk: e.matmul(
                        R.pb[bk][:], wt[:, kc * 128:(kc + 1) * 128], R.hT[:, kc, :],
                        start=(kc == 0), stop=(kc == KC - 1)), r=[wb, R.hT_b[kc]], w=[R.pb_b[bk]])
                xt, xb, xk = R.xs.next()
                P.dma("sp", lambda e, xt=xt, m=m: e.dma_start(out=xt[:], in_=x1T[m * 128:(m + 1) * 128, t0:t0 + TT]),
                      w=[xb], slot=xk)
                ot, ob, ok = R.stf.next()
                P.op("dve", lambda e, ot=ot, xt=xt, bk=bk: e.tensor_tensor(out=ot[:], in0=R.pb[bk][:], in1=xt[:], op=ALU.add),
                     r=[R.pb_b[bk], xb], w=[ob])
                P.dma("sp", lambda e, ot=ot, m=m: e.dma_start(out=x2T[m * 128:(m + 1) * 128, t0:t0 + TT], in_=ot[:]),
                      r=[ob], w=[x2_dst[1][m]], slot=ok)
            x3_dst = (x3T, [Buf() for _ in range(KC)])
            emit_ffn(R, x2_dst, x3_dst, n2_t, n2_b, w1, w2, t0)
            xo_dst = (xoT, [Buf() for _ in range(KC)])
            outs.extend(xo_dst[1])
            emit_norm(R, x3_dst, nf_t, nf_b, t0, dst=xo_dst)

        for t in range(cfg.NTT):
            tile_C(t * TT)
        P.emit(final_bufs=outs)
    return nc


def run_B(cfg, resA, l, inp):
    S, B = cfg.S, cfg.B
    nc = _get_nc(("B", S), lambda: build_B(S))
    per_b = cfg.NTOK // B
    consts = b_consts()

    def full(name, b, axis):
        return np.concatenate([np.asarray(resA[b * per_b + q][name]) for q in range(per_b)], axis=axis)

    conv_w, conv_b = inp["conv_w"][l], inp["conv_b"][l]
    gg = np.ascontiguousarray(inp["gla_out_gain"][l].reshape(2, 128).T)
    in_maps = []
    for b in range(B):
        qsT, ksT, vs = full("qsT", b, 1), full("ksT", b, 1), full("vs", b, 0)
        uT, gqT, gkT, grT = full("uT", b, 1), full("gqT", b, 1), full("gkT", b, 1), full("grT", b, 1)
        gv, gl = full("gv", b, 0), full("gl", b, 0)
        for g in range(4):
            heads = [3 * g + hh for hh in range(3)]
            grps = [2 * g + gi for gi in range(2)]
            ghs = [2 * g + sl if g < 3 else 4 + sl for sl in range(2)]
            m = dict(consts)
            m["qsT"] = np.stack([qsT[h * 128:(h + 1) * 128] for h in heads])
            m["ksT"] = np.stack([ksT[h * 128:(h + 1) * 128] for h in heads])
            m["vs"] = np.stack([lay_tokmajor(vs[:, h * 128:(h + 1) * 128], 128) for h in heads])
            m["uT"] = np.stack([uT[r * 128:(r + 1) * 128] for r in grps])
            m["cw"] = np.stack([np.ascontiguousarray(conv_w[:, r * 128:(r + 1) * 128].T) for r in grps])
            m["cb"] = np.stack([np.ascontiguousarray(conv_b[r * 128:(r + 1) * 128].reshape(128, 1)) for r in grps])
            m["gqT"] = np.stack([gqT[h * 128:(h + 1) * 128] for h in ghs])
            m["gkT"] = np.stack([gkT[h * 128:(h + 1) * 128] for h in ghs])
            m["gv"] = np.stack([lay_tokmajor(gv[:, h * 256:(h + 1) * 256], 256) for h in ghs])
            m["gl"] = np.stack([lay_tokmajor(gl[:, h * 128:(h + 1) * 128], 128) for h in ghs])
            m["grT"] = np.stack([np.ascontiguousarray(grT[h * 256:(h + 1) * 256].reshape(2, 128, S).transpose(1, 0, 2))
                                 for h in ghs])
            m["gg"] = gg
            in_maps.append({k: np.ascontiguousarray(v) for k, v in m.items()})
    res = run_bass_kernel_spmd(nc, in_maps, core_ids=list(range(NCORES))).results
    out = []
    for b in range(B):
        ya = np.concatenate([np.asarray(res[b * 4 + g]["yaT"]).reshape(3 * 128, S) for g in range(4)], axis=0)
        cy = np.concatenate([np.asarray(res[b * 4 + g]["cyT"]).reshape(2 * 128, S) for g in range(4)], axis=0)
        yc = np.concatenate([np.asarray(res[b * 4 + g]["ycT"][sl]).transpose(1, 0, 2).reshape(256, S)
                             for g in range(3) for sl in range(2)], axis=0)
        out.append((ya, cy, yc))
    return out


def run_C(cfg, resA, mix, l, inp):
    nc = _get_nc(("C", cfg.F, cfg.S, cfg.NTOK), lambda: build_C(cfg))
    per_b = cfg.NTOK // cfg.B
    shared = {
        "lng": lay_gain(inp["conv_ln_g"][l]), "lnb": lay_gain(inp["conv_ln_b"][l]),
        "wo": lay_w2(inp["mix_w_out"][l]), "n2": lay_gain(inp["ffn2_norm"][l]),
        "w1": lay_w1(inp["ffn2_w_in"][l], cfg.F), "w2": lay_w2(inp["ffn2_w_out"][l]),
        "nf": lay_gain(inp["final_norm"][l]),
    }
    in_maps = []
    for c in range(cfg.NTOK):
        b, q = divmod(c, per_b)
        ts = slice(q * cfg.NT, (q + 1) * cfg.NT)
        ya, cy, yc = mix[b]
        in_maps.append(dict(shared, x1T=np.asarray(resA[c]["x1T"]),
                            yaT=np.ascontiguousarray(ya[:, ts]), cyT=np.ascontiguousarray(cy[:, ts]),
                            ycT=np.ascontiguousarray(yc[:, ts])))
    return run_bass_kernel_spmd(nc, in_maps, core_ids=list(range(cfg.NTOK))).results


def run_model(inp, cfg):
    inp = {k: np.asarray(v) for k, v in inp.items()}
    xT = tok_shard_T(inp["x"], cfg)
    for l in range(2):
        resA = run_A(cfg, xT, l, inp)
        mix = run_B(cfg, resA, l, inp)
        resC = run_C(cfg, resA, mix, l, inp)
        xT = [np.asarray(r["xoT"]) for r in resC]
    per_b = cfg.NTOK // cfg.B
    out = np.empty((cfg.B, cfg.S, cfg.D), np.float32)
    for c in range(cfg.NTOK):
        b, q = divmod(c, per_b)
        out[b, q * cfg.NT:(q + 1) * cfg.NT, :] = xT[c].T
    return out


def kernel(**inputs):
    return run_model(inputs, FULL)
```
